# Optimizing a Trainium2 kernel written in Bass

```python
import jax
import jax.numpy as jnp
from jax import lax
import numpy as np

D_MODEL = 1024
BATCH = 8
SEQ = 4096
DEPTH = 2
DEC_BATCH = 32
DEC_SEQ = 4
PAST_LEN = 16384
PAGE_SIZE = 128

D_A = D_MODEL // 4
D_B = D_MODEL // 4
D_C = D_MODEL // 2
D_MIX = D_A + D_B + D_C
HEAD_DIM = 64
N_HEADS_C = D_C // HEAD_DIM
CONV_A_WIDTH = 31
CONV_B_WIDTH = 3
FFN_CONV_WIDTH = 3
D_FF = 11 * D_MODEL // 4
DILATED_PATTERNS = ((128, 1), (512, 4), (2048, 16))
MAX_WINDOW = 2048
Q_BLOCK = 128
P_IN = 2 * D_A + 3 * D_B + 3 * D_C
EPS = 1e-6
NEG = -1e30

kernel_name = 'hybrid_conformer_shortconv_dilated_attn_step'


def _rmsnorm(x, g):
    xf = x.astype(jnp.float32)
    y = xf * lax.rsqrt(jnp.mean(xf * xf, axis=-1, keepdims=True) + EPS)
    return (y * g.astype(jnp.float32)).astype(x.dtype)


def _layernorm(x, g, b):
    xf = x.astype(jnp.float32)
    mu = jnp.mean(xf, axis=-1, keepdims=True)
    xc = xf - mu
    var = jnp.mean(xc * xc, axis=-1, keepdims=True)
    return (xc * lax.rsqrt(var + EPS) * g.astype(jnp.float32) + b.astype(jnp.float32)).astype(x.dtype)


def _causal_dwconv(u, state, w):
    k = w.shape[0]
    xp = jnp.concatenate([state.astype(u.dtype), u], axis=1)
    y = lax.conv_general_dilated(
        xp, w[:, None, :].astype(u.dtype), window_strides=(1,), padding='VALID',
        dimension_numbers=('NWC', 'WIO', 'NWC'), feature_group_count=u.shape[-1])
    return y, xp[:, xp.shape[1] - (k - 1):]


def _dilated_attention(q, k, v, q_idx):
    scale = HEAD_DIM ** -0.5
    lses = []
    outs = []
    for window, dil in DILATED_PATTERNS:
        dist = jnp.arange(0, window + 1, dil, dtype=jnp.int32)
        idx = q_idx[:, None] - dist[None, :]
        valid = idx >= 0
        idx = jnp.maximum(idx, 0)
        kg = jnp.take(k, idx, axis=1)
        vg = jnp.take(v, idx, axis=1)
        s = jnp.einsum('bthd,btkhd->bthk', q, kg).astype(jnp.float32) * scale
        s = jnp.where(valid[None, :, None, :], s, NEG)
        lse = jax.nn.logsumexp(s, axis=-1)
        p = jnp.exp(s - lse[..., None])
        outs.append(jnp.einsum('bthk,btkhd->bthd', p.astype(v.dtype), vg).astype(jnp.float32))
        lses.append(lse)
    wts = jax.nn.softmax(jnp.stack(lses), axis=0)
    o = jnp.sum(wts[..., None] * jnp.stack(outs), axis=0)
    return o.astype(q.dtype)


def _layer(x, c, st_a, st_b, st_f, k_past, v_past, lp, is_prompt):
    bsz, t, _ = x.shape
    mod = jax.nn.silu(c) @ lp['w_ada'] + lp['b_ada']
    sh_m, sc_m, ga_m, sh_f, sc_f, ga_f = [m[:, None, :] for m in jnp.split(mod, 6, axis=-1)]

    h = _rmsnorm(x, lp['g_pre_mix']) * (1 + sc_m) + sh_m
    z = h @ lp['w_in']
    splits = [D_A, 2 * D_A, 2 * D_A + D_B, 2 * D_A + 2 * D_B, 2 * D_A + 3 * D_B,
              2 * D_A + 3 * D_B + D_C, 2 * D_A + 3 * D_B + 2 * D_C]
    a_val, a_gate, b_x, b_bg, b_cg, q, k, v = jnp.split(z, splits, axis=-1)

    a = a_val * jax.nn.sigmoid(a_gate)
    a, new_a = _causal_dwconv(a, st_a, lp['conv_a_w'])
    a = jax.nn.silu(_layernorm(a + lp['conv_a_b'], lp['ln_a_g'], lp['ln_a_b']))

    u = b_cg * b_x
    u, new_b = _causal_dwconv(u, st_b, lp['conv_b_w'])
    bo = b_bg * u

    q = q.reshape(bsz, t, N_HEADS_C, HEAD_DIM)
    k = k.reshape(bsz, t, N_HEADS_C, HEAD_DIM)
    v = v.reshape(bsz, t, N_HEADS_C, HEAD_DIM)
    if is_prompt:
        nb = t // Q_BLOCK
        qb = q.reshape(bsz, nb, Q_BLOCK, N_HEADS_C, HEAD_DIM).swapaxes(0, 1)

        def _block(args):
            qi, i = args
            return _dilated_attention(qi, k, v, i * Q_BLOCK + jnp.arange(Q_BLOCK, dtype=jnp.int32))

        o = lax.map(_block, (qb, jnp.arange(nb, dtype=jnp.int32)))
        o = o.swapaxes(0, 1).reshape(bsz, t, D_C)
        keep = min(MAX_WINDOW, t)
        new_k = k[:, t - keep:]
        new_v = v[:, t - keep:]
    else:
        k_all = jnp.concatenate([k_past.astype(k.dtype), k], axis=1)
        v_all = jnp.concatenate([v_past.astype(v.dtype), v], axis=1)
        q_idx = k_past.shape[1] + jnp.arange(t, dtype=jnp.int32)
        o = _dilated_attention(q, k_all, v_all, q_idx).reshape(bsz, t, D_C)
        new_k = k
        new_v = v

    mix = jnp.concatenate([_rmsnorm(a, lp['g_out_a']), _rmsnorm(bo, lp['g_out_b']),
                           _rmsnorm(o, lp['g_out_c'])], axis=-1)
    x = x + ga_m * _rmsnorm(mix @ lp['w_o'], lp['g_post_mix'])

    h = _rmsnorm(x, lp['g_pre_ffn']) * (1 + sc_f) + sh_f
    g, new_f = _causal_dwconv(h @ lp['w_gate'], st_f, lp['conv_f_w'])
    f = jax.nn.silu(g) * (h @ lp['w_up'])
    x = x + ga_f * _rmsnorm(f @ lp['w_down'], lp['g_post_ffn'])
    return x, new_k, new_v, new_a, new_b, new_f


def setup_inputs(seed: int = 0) -> dict:
    key = jax.random.key(seed)
    ks = jax.random.split(key, 32)
    f32 = jnp.float32
    w_buf = min(MAX_WINDOW, PAST_LEN)

    def nrm(k, shape, scale=1.0):
        return jax.random.normal(k, shape, f32) * scale

    def gain(k, shape):
        return 1.0 + 0.02 * jax.random.normal(k, shape, f32)

    return {
        'x_prompt': nrm(ks[0], (BATCH, SEQ, D_MODEL)),
        'x_sample': nrm(ks[1], (DEC_BATCH, DEC_SEQ, D_MODEL)),
        'cache_k': nrm(ks[2], (DEPTH, DEC_BATCH, w_buf, N_HEADS_C, HEAD_DIM)),
        'cache_v': nrm(ks[3], (DEPTH, DEC_BATCH, w_buf, N_HEADS_C, HEAD_DIM)),
        'state_conv_a': nrm(ks[4], (DEPTH, DEC_BATCH, CONV_A_WIDTH - 1, D_A), 0.5),
        'state_conv_b': nrm(ks[5], (DEPTH, DEC_BATCH, CONV_B_WIDTH - 1, D_B), 0.5),
        'state_ffn_conv': nrm(ks[6], (DEPTH, DEC_BATCH, FFN_CONV_WIDTH - 1, D_FF), 0.5),
        'c_prompt': nrm(ks[7], (BATCH, D_MODEL)),
        'c_sample': nrm(ks[8], (DEC_BATCH, D_MODEL)),
        'w_ada': nrm(ks[9], (DEPTH, D_MODEL, 6 * D_MODEL), D_MODEL ** -0.5),
        'b_ada': nrm(ks[10], (DEPTH, 6 * D_MODEL), 0.02),
        'g_pre_mix': gain(ks[11], (DEPTH, D_MODEL)),
        'w_in': nrm(ks[12], (DEPTH, D_MODEL, P_IN), D_MODEL ** -0.5),
        'conv_a_w': nrm(ks[13], (DEPTH, CONV_A_WIDTH, D_A), CONV_A_WIDTH ** -0.5),
        'conv_a_b': nrm(ks[14], (DEPTH, D_A), 0.02),
        'ln_a_g': gain(ks[15], (DEPTH, D_A)),
        'ln_a_b': nrm(ks[16], (DEPTH, D_A), 0.02),
        'conv_b_w': nrm(ks[17], (DEPTH, CONV_B_WIDTH, D_B), CONV_B_WIDTH ** -0.5),
        'g_out_a': gain(ks[18], (DEPTH, D_A)),
        'g_out_b': gain(ks[19], (DEPTH, D_B)),
        'g_out_c': gain(ks[20], (DEPTH, D_C)),
        'w_o': nrm(ks[21], (DEPTH, D_MIX, D_MODEL), D_MIX ** -0.5),
        'g_post_mix': gain(ks[22], (DEPTH, D_MODEL)),
        'g_pre_ffn': gain(ks[23], (DEPTH, D_MODEL)),
        'w_gate': nrm(ks[24], (DEPTH, D_MODEL, D_FF), D_MODEL ** -0.5),
        'w_up': nrm(ks[25], (DEPTH, D_MODEL, D_FF), D_MODEL ** -0.5),
        'conv_f_w': nrm(ks[26], (DEPTH, FFN_CONV_WIDTH, D_FF), FFN_CONV_WIDTH ** -0.5),
        'w_down': nrm(ks[27], (DEPTH, D_FF, D_MODEL), D_FF ** -0.5),
        'g_post_ffn': gain(ks[28], (DEPTH, D_MODEL)),
    }


def reference(x_prompt, x_sample, cache_k, cache_v, state_conv_a, state_conv_b, state_ffn_conv,
              c_prompt, c_sample, w_ada, b_ada, g_pre_mix, w_in, conv_a_w, conv_a_b, ln_a_g, ln_a_b,
              conv_b_w, g_out_a, g_out_b, g_out_c, w_o, g_post_mix, g_pre_ffn, w_gate, w_up,
              conv_f_w, w_down, g_post_ffn):
    xp = x_prompt
    xs = x_sample
    kp_l, vp_l, ks_l, vs_l = [], [], [], []
    ap_l, as_l, bp_l, bs_l, fp_l, fs_l = [], [], [], [], [], []
    for l in range(DEPTH):
        lp = {
            'w_ada': w_ada[l], 'b_ada': b_ada[l], 'g_pre_mix': g_pre_mix[l], 'w_in': w_in[l],
            'conv_a_w': conv_a_w[l], 'conv_a_b': conv_a_b[l], 'ln_a_g': ln_a_g[l], 'ln_a_b': ln_a_b[l],
            'conv_b_w': conv_b_w[l], 'g_out_a': g_out_a[l], 'g_out_b': g_out_b[l], 'g_out_c': g_out_c[l],
            'w_o': w_o[l], 'g_post_mix': g_post_mix[l], 'g_pre_ffn': g_pre_ffn[l],
            'w_gate': w_gate[l], 'w_up': w_up[l], 'conv_f_w': conv_f_w[l], 'w_down': w_down[l],
            'g_post_ffn': g_post_ffn[l],
        }
        z_a = jnp.zeros((xp.shape[0], CONV_A_WIDTH - 1, D_A), xp.dtype)
        z_b = jnp.zeros((xp.shape[0], CONV_B_WIDTH - 1, D_B), xp.dtype)
        z_f = jnp.zeros((xp.shape[0], FFN_CONV_WIDTH - 1, D_FF), xp.dtype)
        xp, kp, vp, ap, bp, fp = _layer(xp, c_prompt, z_a, z_b, z_f, None, None, lp, True)
        xs, ks_, vs_, as_, bs_, fs_ = _layer(xs, c_sample, state_conv_a[l], state_conv_b[l],
                                            state_ffn_conv[l], cache_k[l], cache_v[l], lp, False)
        kp_l.append(kp); vp_l.append(vp); ks_l.append(ks_); vs_l.append(vs_)
        ap_l.append(ap); as_l.append(as_); bp_l.append(bp); bs_l.append(bs_)
        fp_l.append(fp); fs_l.append(fs_)
    return (xp, xs,
            jnp.stack(kp_l), jnp.stack(vp_l), jnp.stack(ks_l), jnp.stack(vs_l),
            jnp.stack(ap_l), jnp.stack(as_l), jnp.stack(bp_l), jnp.stack(bs_l),
            jnp.stack(fp_l), jnp.stack(fs_l))
```

```python
import numpy as np
import concourse.bass as bass
import concourse.mybir as mybir
from concourse.bass_utils import run_bass_kernel_spmd

F32 = mybir.dt.float32
BF16 = mybir.dt.bfloat16
AF = mybir.ActivationFunctionType
ALU = mybir.AluOpType
AX = mybir.AxisListType

D = 1024
KC = 8
PIN = 2816
DFF = 2816
NJ = 22
EPS = 1e-6
NSQ = 4
NST = 4
NS = NSQ * NST
WBUF = 2048
TT = 512

_PL = [("gpm", 8), ("gpostm", 8), ("gpref", 8), ("gpostf", 8), ("bada", 48), ("caw", 62), ("cab", 2),
       ("lng", 2), ("lnb", 2), ("goa", 2), ("cbw", 6), ("gob", 2), ("goc", 4), ("cfw", 66)]
POFF = {}
_o = 0
for _n, _c in _PL:
    POFF[_n] = _o
    _o += _c
PL_COLS = _o
C_IDENT, C_M2, C_MS, C_COLS = 0, 128, 384, 420


class T:
    __slots__ = ("t", "lw", "rd", "name", "excl", "dw")

    def __init__(self, t, name="", excl=False):
        self.t = t
        self.lw = None
        self.rd = {}
        self.name = name
        self.dw = {}
        self.excl = excl

    def __getitem__(self, idx):
        return self.t[idx]


class K:
    def __init__(self, nc, ndma_sems=32):
        self.nc = nc
        self.es = {"pe": nc.tensor, "act": nc.scalar, "dve": nc.vector, "pool": nc.gpsimd, "sp": nc.sync}
        self.sems, self.cnt, self.waited = {}, {}, {}
        self._ctx = []
        for name in self.es:
            self.waited[name] = {}
            self.sems[name] = self._sem("s_" + name)
            self.cnt[name] = 0
        self.dma_ring = []
        for i in range(ndma_sems):
            key = "d%d" % i
            self.sems[key] = self._sem("s_" + key)
            self.cnt[key] = 0
            self.dma_ring.append(key)
        self.dma_next = 0
        self.scopes = []
        self.uid = 0

    def _sem(self, name):
        cm = self.nc.semaphore(name)
        s = cm.__enter__()
        self._ctx.append(cm)
        return s

    def _alloc(self, cm):
        t = cm.__enter__()
        (self.scopes[-1] if self.scopes else self._ctx).append(cm)
        return t

    def sb(self, name, shape, dt):
        self.uid += 1
        name = "%s_u%d" % (name, self.uid)
        return T(self._alloc(self.nc.sbuf_tensor(name, list(shape), dt)), name)

    def ps(self, name, shape, dt=F32):
        self.uid += 1
        name = "%s_u%d" % (name, self.uid)
        return T(self._alloc(self.nc.psum_tensor(name, list(shape), dt)), name, excl=True)

    def view(self, t, name=""):
        return T(t.t, name)

    def push(self):
        self.scopes.append([])

    def pop(self):
        self.barrier()
        for cm in reversed(self.scopes.pop()):
            cm.__exit__(None, None, None)

    def close(self):
        for cm in reversed(self._ctx):
            cm.__exit__(None, None, None)
        self._ctx = []

    def _need(self, reads, writes, par=False):
        need = {}

        def upd(k_, v):
            if need.get(k_, 0) < v:
                need[k_] = v
        for t in reads:
            if t.lw is not None:
                upd(*t.lw)
            for k_, v in t.dw.items():
                upd(k_, v)
            if t.excl:
                for k_, v in t.rd.items():
                    upd(k_, v)
        for t in writes:
            if t.lw is not None:
                upd(*t.lw)
            if not par:
                for k_, v in t.dw.items():
                    upd(k_, v)
            for k_, v in t.rd.items():
                upd(k_, v)
        return need

    def _wait(self, eng, need):
        w = self.waited[eng]
        h = self.es[eng]
        for k_, v in need.items():
            if k_ == eng and v > self.cnt[eng]:
                continue
            if w.get(k_, 0) < v:
                h.wait_ge(self.sems[k_], v)
                w[k_] = v

    def _record(self, ev, reads, writes, par=False):
        k_, v = ev
        for t in reads:
            if t.rd.get(k_, 0) < v:
                t.rd[k_] = v
        for t in writes:
            if par:
                t.dw[k_] = v
            else:
                t.lw = ev
                t.dw = {}
            t.rd = {}

    def op(self, eng, fn, reads=(), writes=(), inc=True):
        self._wait(eng, self._need(reads, writes))
        inst = fn()
        if inc:
            inst.then_inc(self.sems[eng], 1)
            self.cnt[eng] += 1
            ev = (eng, self.cnt[eng])
        else:
            ev = (eng, self.cnt[eng] + 1)
        self._record(ev, reads, writes)
        return inst

    def dma(self, q, out, in_, reads=(), writes=(), par=False, **kw):
        need = self._need(reads, writes, par)
        key = self.dma_ring[self.dma_next]
        self.dma_next = (self.dma_next + 1) % len(self.dma_ring)
        if self.cnt[key] > 0:
            need[key] = max(need.get(key, 0), self.cnt[key])
        self._wait(q, need)
        inst = self.es[q].dma_start(out=out, in_=in_, **kw)
        inst.then_inc(self.sems[key], 16)
        self.cnt[key] += 16
        self._record((key, self.cnt[key]), reads, writes, par)
        return inst

    def barrier(self):
        need = {}
        for k_ in self.dma_ring:
            if self.cnt[k_] > 0:
                need[k_] = self.cnt[k_]
        for e in self.es:
            if self.cnt[e] > 0:
                need[e] = self.cnt[e]
        for e in self.es:
            n2 = {k_: v for k_, v in need.items() if k_ != e}
            self._wait(e, n2)


def AP(t, off, dims):
    h = t.t if isinstance(t, T) else t
    return bass.AP(h, off, [list(d) for d in dims])


class _Stop(Exception):
    pass


def build(S=4096, stop_after=None, debug=None):
    try:
        return _build(S, stop_after, debug)
    except _Stop as e:
        k, nc = e.args
        k.scopes = []
        k.barrier()
        return nc


def _build(S=4096, stop_after=None, debug=None):
    nc = bass.Bass("TRN2", target_bir_lowering=False)
    NTILE = S // TT
    NHALF = S // 2048
    KEEP = min(2048, S)

    def din(name, shape, dt=F32):
        return nc.dram_tensor(name, list(shape), dt, kind="ExternalInput").ap()

    def dout(name, shape, dt=F32):
        return nc.dram_tensor(name, list(shape), dt, kind="ExternalOutput").ap()

    def dscr(name, shape, dt):
        return nc.dram_tensor(name, list(shape), dt, kind="Internal").ap()

    xT = din("xT", [D, S])
    xsT = din("xsT", [D, NS])
    cT = din("cT", [128, KC, 5])
    prm = din("prm", [128, 2 * PL_COLS])
    cst = din("cst", [128, C_COLS])
    ck = din("ck", [2, NSQ, WBUF, 512])
    cv = din("cv", [2, NSQ, WBUF, 512])
    sa = din("sa", [128, 2, 2, NSQ, 30])
    sbs = din("sbs", [128, 2, 2, NSQ, 2])
    sf = din("sf", [128, 2, NJ, NSQ, 2])
    w_ada = din("w_ada", [2, D, 6 * D])
    w_in = din("w_in", [2, D, PIN])
    w_o = din("w_o", [2, D, D])
    w_gate = din("w_gate", [2, D, DFF])
    w_up = din("w_up", [2, D, DFF])
    w_down = din("w_down", [2, DFF, D])
    yT = dout("yT", [D, S])
    ysT = dout("ysT", [D, NS])
    kpT = dout("kpT", [2, 512, KEEP])
    vpT = dout("vpT", [2, 512, KEEP])
    ksT = dout("ksT", [2, 512, NS])
    vsT = dout("vsT", [2, 512, NS])
    apT = dout("apT", [2, 256, 30])
    asT = dout("asT", [2, 256, NSQ, 30])
    bpT = dout("bpT", [2, 256, 2])
    bsT = dout("bsT", [2, 256, NSQ, 2])
    fpT = dout("fpT", [2, DFF, 2])
    fsT = dout("fsT", [2, DFF, NSQ, 2])
    qkvT = dscr("qkvT", [12, 128, S], BF16)
    mixT = dscr("mixT", [8, 128, S], BF16)
    x1T = dscr("x1T", [D, S], F32)
    x2T = dscr("x2T", [D, S], F32)
    win_bf = dscr("win_bf", [2, D, PIN], BF16)
    wo_bf = dscr("wo_bf", [2, D, D], BF16)
    wg_bf = dscr("wg_bf", [2, D, DFF], BF16)
    wu_bf = dscr("wu_bf", [2, D, DFF], BF16)
    wd_bf = dscr("wd_bf", [2, DFF, D], BF16)
    h2T = dscr("h2T", [8, 128, S], BF16)

    k = K(nc)
    act, dve, pool, pe = nc.scalar, nc.vector, nc.gpsimd, nc.tensor

    Pm = k.sb("Pm", [128, 2 * PL_COLS], F32)
    Cf = k.sb("Cf", [128, C_COLS], F32)
    identb = k.sb("identb", [128, 128], BF16)
    M2b = k.sb("M2b", [128, 2, 128], BF16)
    Msb = k.sb("Msb", [128, 9, NST], BF16)
    onesb = k.sb("onesb", [128, 128], BF16)
    cTs = k.sb("cTs", [128, KC, 5], F32)
    scb = k.sb("scb", [128, KC, 5], BF16)
    scf = k.sb("scf", [128, KC, 5], F32)
    modT = k.sb("modT", [128, 2, 48, 5], F32)
    mv = k.sb("mv", [128, 6, KC, 5], F32)
    mvs = k.sb("mvs", [128, 6, KC, NS], F32)
    xs = k.sb("xs", [128, KC, NS], F32)
    bigW = k.sb("bigW", [128, KC * PIN], BF16)

    def P(name, l, c0=0, n=1):
        o = l * PL_COLS + POFF[name] + c0
        return Pm[:, o:o + n]

    k.dma("sp", Pm[:], prm[:, :], writes=[Pm])
    k.dma("sp", Cf[:], cst[:, :], writes=[Cf])
    k.dma("sp", cTs[:], cT[:, :, :], writes=[cTs])
    k.dma("sp", xs[:], xsT.rearrange("(kc p) n -> p kc n", p=128), writes=[xs])
    k.op("dve", lambda: dve.tensor_copy(out=identb[:], in_=Cf[:, C_IDENT:C_IDENT + 128]), [Cf], [identb])
    k.op("dve", lambda: dve.tensor_copy(out=M2b[:].rearrange("p a b -> p (a b)"), in_=Cf[:, C_M2:C_M2 + 256]), [Cf], [M2b])
    k.op("dve", lambda: dve.tensor_copy(out=Msb[:].rearrange("p a b -> p (a b)"), in_=Cf[:, C_MS:C_MS + 36]), [Cf], [Msb])
    k.op("dve", lambda: dve.memset(onesb[:], 1.0), [], [onesb])
    k.op("act", lambda: act.activation(out=scb[:], in_=cTs[:], func=AF.Silu), [cTs], [scb])
    k.op("act", lambda: act.activation(out=scf[:], in_=cTs[:], func=AF.Silu), [cTs], [scf])

    Twin = [T(None, "twin%d" % i) for i in range(2)]
    Two = [T(None, "two%d" % i) for i in range(2)]
    Twg = [T(None, "twg%d" % i) for i in range(2)]
    Twu = [T(None, "twu%d" % i) for i in range(2)]
    Twd = [T(None, "twd%d" % i) for i in range(2)]

    def cast_weights(l):
        def c2(dst, src, tt, nsplit, cols):
            R = src.shape[0]
            step = R // nsplit
            for i in range(nsplit):
                if cols > 2048:
                    k.dma("pool", dst[i * step:(i + 1) * step, :].rearrange("r (a c) -> r a c", a=2),
                          src[i * step:(i + 1) * step, :].rearrange("r (a c) -> r a c", a=2), writes=[tt], par=True)
                else:
                    k.dma("pool", dst[i * step:(i + 1) * step, :], src[i * step:(i + 1) * step, :], writes=[tt], par=True)
        if l == 0:
            return
        c2(win_bf[l], w_in[l], Twin[l], 2, PIN)
        cast_rest(l)

    def cast_rest(l):
        def c2(dst, src, tt, nsplit, cols):
            R = src.shape[0]
            step = R // nsplit
            for i in range(nsplit):
                if cols > 2048:
                    k.dma("pool", dst[i * step:(i + 1) * step, :].rearrange("r (a c) -> r a c", a=2),
                          src[i * step:(i + 1) * step, :].rearrange("r (a c) -> r a c", a=2), writes=[tt], par=True)
                else:
                    k.dma("pool", dst[i * step:(i + 1) * step, :], src[i * step:(i + 1) * step, :], writes=[tt], par=True)
        c2(wo_bf[l], w_o[l], Two[l], 1, D)
        c2(wd_bf[l], w_down[l], Twd[l], 2, D)
        c2(wg_bf[l], w_gate[l], Twg[l], 2, DFF)
        c2(wu_bf[l], w_up[l], Twu[l], 2, DFF)

    if stop_after == "pre":
        k.barrier(); k.close(); return nc
    class M0:
        def __init__(self, l, nb, bf=False, gw=128):
            self.l, self.nb, self.bf, self.gw = l, nb, bf, gw
            self.ng = 6 * D // gw
            self.buf = [k.sb("wabf%d" % i, [128, KC, gw], F32) for i in range(nb)]
            self.bufb = [k.sb("wabb%d" % i, [128, KC, gw], BF16) for i in range(nb)] if bf else None
            self.psm = k.ps("psm", [128, 512])
            self.nl = 0
            self.nm = 0

        def load(self):
            if self.nl >= self.ng:
                return
            g = self.nl
            self.nl += 1
            wf = self.buf[g % self.nb]
            src = w_ada[self.l].rearrange("(kc p) n -> p kc n", p=128)[:, :, g * self.gw:(g + 1) * self.gw]
            k.dma("sp", wf[:], src, writes=[wf])
            if self.bf:
                wb = self.bufb[g % self.nb]
                k.op("dve", lambda: dve.tensor_copy(out=wb[:], in_=wf[:]), [wf], [wb])

        def mm(self):
            if self.nm >= self.ng:
                return
            g, l = self.nm, self.l
            self.nm += 1
            wf = self.bufb[g % self.nb] if self.bf else self.buf[g % self.nb]
            sc_ = scb if self.bf else scf
            for fl in range(self.gw // 128):
                fc = g * (self.gw // 128) + fl
                for kc in range(KC):
                    k.op("pe", lambda: pe.matmul(self.psm[:, fc * 5:(fc + 1) * 5], lhsT=wf[:, kc, fl * 128:(fl + 1) * 128],
                                                 rhs=sc_[:, kc, :], start=(kc == 0), stop=(kc == KC - 1)), [wf, sc_], [self.psm],
                         inc=(kc == KC - 1))
            if self.nm == self.ng:
                o = l * PL_COLS + POFF["bada"]
                bb = AP(Pm, o, [[2 * PL_COLS, 128], [1, 48], [0, 5]])
                k.op("dve", lambda: dve.tensor_tensor(out=modT[:, l, :, :], in0=self.psm[:, 0:240].rearrange("p (a b) -> p a b", b=5),
                                                      in1=bb, op=ALU.add), [self.psm, Pm], [modT])

    k.push()
    m0 = M0(0, 4, bf=True, gw=512)
    for g in range(3):
        m0.load()
    wst = [k.sb("wst%d" % i, [128, PIN], F32) for i in range(2)]
    for kc in range(KC):
        k.dma("sp", wst[kc % 2][:], w_in[0, kc * 128:(kc + 1) * 128, :], writes=[wst[kc % 2]])
        if kc % 2 == 0:
            k.op("act", lambda: act.activation(out=bigW[:, kc * PIN:(kc + 1) * PIN], in_=wst[kc % 2][:], func=AF.Copy),
                 [wst[kc % 2]], [bigW])
        else:
            k.op("pool", lambda: pool.tensor_copy(out=bigW[:, kc * PIN:(kc + 1) * PIN], in_=wst[kc % 2][:]), [wst[kc % 2]], [bigW])
    for g in range(48):
        m0.load()
        m0.mm()
    k.pop()

    class Caster:
        def __init__(self, pieces):
            self.p = pieces
            self.s32 = [k.sb("cst32_%d" % i, [128, 1408], F32) for i in range(2)]
            self.s16 = [k.sb("cst16_%d" % i, [128, 1408], BF16) for i in range(2)]
            self.i = 0

        def step(self):
            i = self.i
            self.i += 1
            n = len(self.p)
            if i - 2 >= 0 and i - 2 < n:
                src, dst, tt, w = self.p[i - 2]
                k.dma("sp", dst, self.s16[i % 2][:, 0:w], reads=[self.s16[i % 2]], writes=[tt], par=True)
            if i < n:
                src, dst, tt, w = self.p[i]
                k.dma("sp", self.s32[i % 2][:, 0:w], src, writes=[self.s32[i % 2]])
            if i - 1 >= 0 and i - 1 < n:
                src, dst, tt, w = self.p[i - 1]
                a, b = self.s32[(i - 1) % 2], self.s16[(i - 1) % 2]
                k.op("act", lambda: act.activation(out=b[:, 0:w], in_=a[:, 0:w], func=AF.Copy), [a], [b])

        def done(self):
            return self.i >= len(self.p) + 2

    def cast_pieces(l):
        ps_ = []
        for kc in range(KC):
            ps_.append((w_o[l, kc * 128:(kc + 1) * 128, :], wo_bf[l, kc * 128:(kc + 1) * 128, :], Two[l], D))
        for j in range(NJ):
            ps_.append((w_down[l, j * 128:(j + 1) * 128, :], wd_bf[l, j * 128:(j + 1) * 128, :], Twd[l], D))
        for (wsrc, wdst, tt) in ((w_gate, wg_bf, Twg), (w_up, wu_bf, Twu)):
            for kc in range(KC):
                for hh in range(2):
                    ps_.append((wsrc[l, kc * 128:(kc + 1) * 128, hh * 1408:(hh + 1) * 1408],
                                wdst[l, kc * 128:(kc + 1) * 128, hh * 1408:(hh + 1) * 1408], tt[l], 1408))
        if l + 1 < 2:
            for kc in range(KC):
                for hh in range(2):
                    ps_.append((w_in[l + 1, kc * 128:(kc + 1) * 128, hh * 1408:(hh + 1) * 1408],
                                win_bf[l + 1, kc * 128:(kc + 1) * 128, hh * 1408:(hh + 1) * 1408], Twin[l + 1], 1408))
        return ps_

    if stop_after == "M0":
        k.barrier(); k.close(); return nc

    def setup_layer_mod(l):
        def m(i0):
            return modT[:, l, i0 * 8:(i0 + 1) * 8, :]
        def pb(name):
            o = l * PL_COLS + POFF[name]
            return AP(Pm, o, [[2 * PL_COLS, 128], [1, 8], [0, 5]])
        k.op("dve", lambda: dve.scalar_tensor_tensor(out=mv[:, 0, :, :], in0=m(1), scalar=1.0, in1=pb("gpm"),
                                                     op0=ALU.add, op1=ALU.mult), [modT, Pm], [mv])
        k.op("dve", lambda: dve.tensor_copy(out=mv[:, 1, :, :], in_=m(0)), [modT], [mv])
        k.op("dve", lambda: dve.tensor_tensor(out=mv[:, 2, :, :], in0=m(2), in1=pb("gpostm"), op=ALU.mult), [modT, Pm], [mv])
        k.op("dve", lambda: dve.scalar_tensor_tensor(out=mv[:, 3, :, :], in0=m(4), scalar=1.0, in1=pb("gpref"),
                                                     op0=ALU.add, op1=ALU.mult), [modT, Pm], [mv])
        k.op("dve", lambda: dve.tensor_copy(out=mv[:, 4, :, :], in_=m(3)), [modT], [mv])
        k.op("dve", lambda: dve.tensor_tensor(out=mv[:, 5, :, :], in0=m(5), in1=pb("gpostf"), op=ALU.mult), [modT, Pm], [mv])
        for i in range(6):
            src = AP(mv, (i * KC * 5) + 1, [[6 * KC * 5, 128], [5, KC], [1, NSQ], [0, NST]])
            k.op("dve", lambda: dve.tensor_copy(out=mvs[:, i, :, :].rearrange("p k (s t) -> p k s t", t=NST), in_=src),
                 [mv], [mvs])

    def rstd_bcast(sq_ap_list, n, nfeat, ps_t, lnv, rstd, reads):
        m = len(sq_ap_list)
        for i, a in enumerate(sq_ap_list):
            k.op("pe", lambda: pe.matmul(ps_t[:, 0:n], lhsT=onesb[:], rhs=a, start=(i == 0), stop=(i == m - 1)),
                 reads + [onesb], [ps_t], inc=(i == m - 1))
        k.op("act", lambda: act.activation(out=lnv[:, 0:n], in_=ps_t[:, 0:n], func=AF.Ln, bias=EPS, scale=1.0 / nfeat),
             [ps_t], [lnv])
        k.op("act", lambda: act.activation(out=rstd[:, 0:n], in_=lnv[:, 0:n], func=AF.Exp, scale=-0.5), [lnv], [rstd])

    def chk(n):
        if debug is not None and debug == n:
            raise _Stop(k, nc)

    def store(dst, src, reads):
        k.dma("sp", dst, src, reads=reads)

    zs = k.sb("zs", [128, NJ, NS], F32)
    qs = k.sb("qs", [128, 4, NS], BF16)
    ksn = k.sb("ksn", [128, 4, NS], BF16)
    vsn = k.sb("vsn", [128, 4, NS], BF16)
    mixs = k.sb("mixs", [128, KC, NS], BF16)
    tms = k.sb("tms", [128, KC, NS], F32)
    sqs = k.sb("sqs", [128, KC, NS], BF16)
    hts = k.sb("hts", [128, KC, NS], BF16)
    ghalo = k.sb("ghalo", [128, NJ, 2], F32)
    xpf = k.sb("xpf", [128, NJ, NSQ, 6], F32)

    def prenorm(xtile, xap, n, sq_t, ht_t, gsi, shi, sample, psn_t, lnv_t, rstd_t, tmpl, phase=0):
        if phase in (0, 1):
            k.op("act", lambda: act.activation(out=sq_t[:, :, 0:n], in_=xap(None), func=AF.Square), [xtile], [sq_t])
        if phase == 1:
            return
        rstd_bcast([sq_t[:, kc, 0:n] for kc in range(KC)], n, D, psn_t, lnv_t, rstd_t, [sq_t])
        if not sample:
            for kc in range(KC):
                tm = tmpl[kc % len(tmpl)]
                k.op("dve", lambda: dve.scalar_tensor_tensor(out=tm[:, 0:n], in0=xap(kc), scalar=mv[:, gsi, kc, 0:1],
                                                             in1=rstd_t[:, 0:n], op0=ALU.mult, op1=ALU.mult),
                     [xtile, mv, rstd_t], [tm])
                k.op("act", lambda: act.activation(out=ht_t[:, kc, 0:n], in_=tm[:, 0:n], func=AF.Identity,
                                                   bias=mv[:, shi, kc, 0:1], scale=1.0), [tm, mv], [ht_t])
        else:
            rb = AP(rstd_t, 0, [[TT, 128], [0, KC], [1, n]])
            k.op("dve", lambda: dve.tensor_tensor(out=tms[:], in0=xs[:], in1=rb, op=ALU.mult), [xs, rstd_t], [tms])
            k.op("dve", lambda: dve.tensor_tensor(out=tms[:], in0=tms[:], in1=mvs[:, gsi, :, :], op=ALU.mult), [tms, mvs], [tms])
            k.op("dve", lambda: dve.tensor_tensor(out=ht_t[:], in0=tms[:], in1=mvs[:, shi, :, :], op=ALU.add), [tms, mvs], [ht_t])

    def postnorm_resid(n, o32, osq, psn_t, lnv_t, rstd_t, ggi, xtile, xap, sample, o32T=None, osqT=None):
        o32T = o32T or [o32] * KC
        osqT = osqT or [osq] * KC
        rstd_bcast([osq[:, dc, 0:n] for dc in range(KC)], n, D, psn_t, lnv_t, rstd_t, list(osqT))
        if not sample:
            for dc in range(KC):
                k.op("dve", lambda: dve.scalar_tensor_tensor(out=o32[:, dc, 0:n], in0=o32[:, dc, 0:n], scalar=mv[:, ggi, dc, 0:1],
                                                             in1=rstd_t[:, 0:n], op0=ALU.mult, op1=ALU.mult),
                     [o32T[dc], mv, rstd_t], [o32T[dc]])
                k.op("dve", lambda: dve.tensor_tensor(out=xap(dc), in0=xap(dc), in1=o32[:, dc, 0:n], op=ALU.add),
                     [xtile, o32T[dc]], [xtile])
        else:
            rb = AP(rstd_t, 0, [[TT, 128], [0, KC], [1, n]])
            allo = list(dict.fromkeys(o32T))
            k.op("dve", lambda: dve.tensor_tensor(out=o32[:, :, 0:n], in0=o32[:, :, 0:n], in1=rb, op=ALU.mult), allo + [rstd_t], allo)
            k.op("dve", lambda: dve.tensor_tensor(out=o32[:, :, 0:n], in0=o32[:, :, 0:n], in1=mvs[:, ggi, :, :], op=ALU.mult),
                 allo + [mvs], allo)
            k.op("dve", lambda: dve.tensor_tensor(out=xs[:], in0=xs[:], in1=o32[:, :, 0:n], op=ALU.add), [xs] + allo, [xs])

    for l in range(2):
        xin = xT if l == 0 else x2T
        xout = x2T if l == 0 else yT
        setup_layer_mod(l)
        if l > 0:
            wv = win_bf[l].rearrange("(kc p) n -> p kc n", p=128)
            for kc in range(KC):
                k.dma("sp", bigW[:, kc * PIN:(kc + 1) * PIN], wv[:, kc, :], reads=[Twin[l]], writes=[bigW], par=True)

        def win(kc, c0, n):
            return bigW[:, kc * PIN + c0: kc * PIN + c0 + n]

        k.push()
        xt = k.sb("xt", [128, KC, TT], F32)
        sqb = k.sb("sqb", [128, KC, TT], BF16)
        ht = [k.sb("ht%d" % i, [128, KC, TT], BF16) for i in range(2)]
        lnv = k.sb("lnv", [128, TT], F32)
        rstd = k.sb("rstd", [128, TT], F32)
        tmpf = [k.sb("tmpf%d" % i, [128, TT], F32) for i in range(2)]
        sg = [k.sb("sg%d" % i, [128, TT], F32) for i in range(2)]
        abuf = [[k.sb("abuf%d_%d" % (r, c), [128, 30 + TT], BF16) for c in range(2)] for r in range(2)]
        a32l = k.sb("a32l", [128, 2, 30], F32)
        Dg = k.sb("Dg", [128, 2, 31, 128], BF16)
        y32 = k.sb("y32", [128, 2, TT], F32)
        ybf = k.sb("ybf", [128, 2, TT], BF16)
        ysq = k.sb("ysq", [128, 2, TT], BF16)
        mean = k.sb("mean", [128, TT], F32)
        var = k.sb("var", [128, TT], F32)
        lnv2 = k.sb("lnv2", [128, TT], F32)
        rstd2 = k.sb("rstd2", [128, TT], F32)
        ta = k.sb("ta", [128, 2, TT], F32)
        sa_ = k.sb("sa_", [128, 2, TT], F32)
        ssq = k.sb("ssq", [128, 2, TT], BF16)
        lnv3 = k.sb("lnv3", [128, TT], F32)
        rstd3 = k.sb("rstd3", [128, TT], F32)
        mixab = [k.sb("mixab%d" % i, [128, 4, TT], BF16) for i in range(2)]
        bx = [k.sb("bx%d" % i, [128, TT], F32) for i in range(2)]
        u32 = [k.sb("u32_%d" % i, [128, 2, 2 + TT], F32) for i in range(2)]
        cacc = [k.sb("cacc%d" % i, [128, TT], F32) for i in range(2)]
        bo = k.sb("bo", [128, 2, TT], F32)
        bsq = k.sb("bsq", [128, 2, TT], BF16)
        qkb = [k.sb("qkb%d" % i, [128, TT], BF16) for i in range(4)]
        kv32 = [k.sb("kv32_%d" % i, [128, TT], F32) for i in range(3)]
        sgs = k.sb("sgs", [128, 2, NS], F32)
        xpa = k.sb("xpa", [128, 2, NSQ, 34], F32)
        xpb = k.sb("xpb", [128, 2, NSQ, 6], F32)
        tmpc = k.sb("tmpc", [128, NS * 31], F32)
        ysm = k.sb("ysm", [128, 2, NS], F32)
        cas = k.sb("cas", [128, 2, NS], F32)
        psn = k.ps("psn", [128, TT])
        psz = [k.ps("psz%d" % i, [128, TT]) for i in range(4)]
        psy = [k.ps("psy%d" % i, [128, TT]) for i in range(2)]
        pss = k.ps("pss", [128, TT])

        for c in range(2):
            for j in range(31):
                k.op("act", lambda: act.activation(out=Dg[:, c, j, :], in_=identb[:], func=AF.Copy,
                                                   scale=P("caw", l, c * 31 + j)), [identb, Pm], [Dg])
        for r in range(2):
            for c in range(2):
                k.op("pool", lambda: pool.memset(abuf[r][c][:, 0:30], 0.0), [], [abuf[r][c]])
            k.op("pool", lambda: pool.memset(u32[r][:, :, 0:2], 0.0), [], [u32[r]])
        k.dma("sp", xpa[:, :, :, 0:30], sa[:, l, :, :, :], writes=[xpa])
        k.dma("sp", xpb[:, :, :, 0:2], sbs[:, l, :, :, :], writes=[xpb])

        def a_tail(n, out_t, out_ap, part=0):
            if part in (0, 1):
                a_tail_1(n)
            if part in (0, 2):
                a_tail_2(n, out_t, out_ap)

        def a_tail_2(n, out_t, out_ap):
            rstd_bcast([ssq[:, c, 0:n] for c in range(2)], n, 256, psn, lnv3, rstd3, [ssq])
            for c in range(2):
                k.op("dve", lambda: dve.scalar_tensor_tensor(out=out_ap(c), in0=sa_[:, c, 0:n], scalar=P("goa", l, c),
                                                             in1=rstd3[:, 0:n], op0=ALU.mult, op1=ALU.mult),
                     [sa_, Pm, rstd3], [out_t])

        def a_tail_1(n):
            pm_, pq_ = psy[0], psy[1]
            for c in range(2):
                k.op("pe", lambda: pe.matmul(pm_[:, 0:n], lhsT=onesb[:], rhs=ybf[:, c, 0:n], start=(c == 0), stop=(c == 1)),
                     [onesb, ybf], [pm_], inc=(c == 1))
            for c in range(2):
                k.op("pe", lambda: pe.matmul(pq_[:, 0:n], lhsT=onesb[:], rhs=ysq[:, c, 0:n], start=(c == 0), stop=(c == 1)),
                     [onesb, ysq], [pq_], inc=(c == 1))
            k.op("act", lambda: act.activation(out=mean[:, 0:n], in_=pm_[:, 0:n], func=AF.Copy, scale=1.0 / 256), [pm_], [mean])
            k.op("dve", lambda: dve.tensor_tensor(out=var[:, 0:n], in0=mean[:, 0:n], in1=mean[:, 0:n], op=ALU.mult), [mean], [var])
            k.op("dve", lambda: dve.scalar_tensor_tensor(out=var[:, 0:n], in0=pq_[:, 0:n], scalar=1.0 / 256, in1=var[:, 0:n],
                                                         op0=ALU.mult, op1=ALU.subtract), [pq_, var], [var])
            k.op("act", lambda: act.activation(out=lnv2[:, 0:n], in_=var[:, 0:n], func=AF.Ln, bias=EPS, scale=1.0), [var], [lnv2])
            k.op("act", lambda: act.activation(out=rstd2[:, 0:n], in_=lnv2[:, 0:n], func=AF.Exp, scale=-0.5), [lnv2], [rstd2])
            for c in range(2):
                k.op("dve", lambda: dve.tensor_tensor(out=ta[:, c, 0:n], in0=y32[:, c, 0:n], in1=mean[:, 0:n], op=ALU.subtract),
                     [y32, mean], [ta])
                k.op("dve", lambda: dve.tensor_tensor(out=ta[:, c, 0:n], in0=ta[:, c, 0:n], in1=rstd2[:, 0:n], op=ALU.mult),
                     [ta, rstd2], [ta])
                k.op("act", lambda: act.activation(out=sa_[:, c, 0:n], in_=ta[:, c, 0:n], func=AF.Silu, bias=P("lnb", l, c),
                                                   scale=P("lng", l, c)), [ta, Pm], [sa_])
                k.op("act", lambda: act.activation(out=ssq[:, c, 0:n], in_=sa_[:, c, 0:n], func=AF.Square), [sa_], [ssq])

        def b_tail(n, out_t, out_ap, part=0):
            if part in (0, 1):
                for c in range(2):
                    k.op("act", lambda: act.activation(out=bsq[:, c, 0:n], in_=bo[:, c, 0:n], func=AF.Square), [bo], [bsq])
            if part == 1:
                return
            rstd_bcast([bsq[:, c, 0:n] for c in range(2)], n, 256, psn, lnv3, rstd3, [bsq])
            for c in range(2):
                k.op("dve", lambda: dve.scalar_tensor_tensor(out=out_ap(c), in0=bo[:, c, 0:n], scalar=P("gob", l, c),
                                                             in1=rstd3[:, 0:n], op0=ALU.mult, op1=ALU.mult),
                     [bo, Pm, rstd3], [out_t])

        prenorm(xs, lambda kc: xs[:] if kc is None else xs[:, kc, :], NS, sqs, hts, 0, 1, True, psn, lnv, rstd, tmpf)

        def m1_tails(t_, part=0):
            mx_ = mixab[t_ % 2]
            a_tail(TT, mx_, lambda c: mx_[:, c, :], part)
            b_tail(TT, mx_, lambda c: mx_[:, 2 + c, :], part)
            if part in (0, 2):
                k.dma("sp", mixT[0:4, :, t_ * TT:(t_ + 1) * TT].rearrange("c p n -> p c n"), mx_[:], reads=[mx_])

        def m1_pre(t_, phase=0):
            if phase in (0, -1):
                k.dma("sp", xt[:], xin.rearrange("(kc p) n -> p kc n", p=128)[:, :, t_ * TT:(t_ + 1) * TT], writes=[xt])
            if phase == -1:
                return
            prenorm(xt, lambda kc: xt[:] if kc is None else xt[:, kc, :], TT, sqb, ht[t_ % 2], 0, 1, False, psn, lnv, rstd, tmpf,
                    phase)

        for t in range(NTILE):
            t0 = t * TT
            htt = ht[t % 2]
            ab = abuf[t % 2]
            abp = abuf[(t + 1) % 2]
            uu = u32[t % 2]
            uup = u32[(t + 1) % 2]
            mx = mixab[t % 2]
            if t == 0:
                m1_pre(0)
            if t + 1 < NTILE:
                m1_pre(t + 1, -1)

            def inproj(oc, ps_t):
                for kc in range(KC):
                    k.op("pe", lambda: pe.matmul(ps_t[:], lhsT=win(kc, oc * 128, 128), rhs=htt[:, kc, :],
                                                 start=(kc == 0), stop=(kc == KC - 1)), [bigW, htt], [ps_t], inc=(kc == KC - 1))
                if t == 0:
                    for kc in range(KC):
                        k.op("pe", lambda: pe.matmul(pss[:, 0:NS], lhsT=win(kc, oc * 128, 128), rhs=hts[:, kc, :],
                                                     start=(kc == 0), stop=(kc == KC - 1)), [bigW, hts], [pss], inc=(kc == KC - 1))
                    k.op("dve", lambda: dve.tensor_copy(out=zs[:, oc, :], in_=pss[:, 0:NS]), [pss], [zs])

            for c in range(2):
                pv, pg = psz[2 * c], psz[2 * c + 1]
                inproj(c, pv)
                inproj(2 + c, pg)
                s_ = sg[c]
                k.op("act", lambda: act.activation(out=s_[:], in_=pg[:], func=AF.Sigmoid), [pg], [s_])
                if t > 0:
                    k.op("pool", lambda: pool.tensor_copy(out=ab[c][:, 0:30], in_=abp[c][:, TT:TT + 30]), [abp[c]], [ab[c]])
                k.op("dve", lambda: dve.tensor_tensor(out=ab[c][:, 30:30 + TT], in0=pv[:], in1=s_[:], op=ALU.mult),
                     [pv, s_], [ab[c]])
                if t == NTILE - 1:
                    k.op("dve", lambda: dve.tensor_tensor(out=a32l[:, c, :], in0=pv[:, TT - 30:TT], in1=s_[:, TT - 30:TT],
                                                          op=ALU.mult), [pv, s_], [a32l])
            if t > 0:
                m1_tails(t - 1, 1)
            for c in range(2):
                for j in range(31):
                    k.op("pe", lambda: pe.matmul(psy[c][:], lhsT=Dg[:, c, j, :], rhs=ab[c][:, j:j + TT], start=(j == 0),
                                                 stop=(j == 30)), [Dg, ab[c]], [psy[c]], inc=(j == 30))
                cb = P("cab", l, c)
                k.op("act", lambda: act.activation(out=y32[:, c, :], in_=psy[c][:], func=AF.Identity, bias=cb, scale=1.0),
                     [psy[c], Pm], [y32])
                k.op("act", lambda: act.activation(out=ysq[:, c, :], in_=psy[c][:], func=AF.Square, bias=cb, scale=1.0),
                     [psy[c], Pm], [ysq])
                k.op("dve", lambda: dve.tensor_copy(out=ybf[:, c, :], in_=y32[:, c, :]), [y32], [ybf])
            if t == NTILE - 1:
                store(apT[l].rearrange("(c p) n -> p c n", p=128), a32l[:], [a32l])
            if t > 0:
                m1_tails(t - 1, 2)
            if t + 1 < NTILE:
                m1_pre(t + 1, 1)
            for c in range(2):
                px, pbg, pcg = psz[0], psz[1], psz[2]
                inproj(4 + c, px)
                inproj(6 + c, pbg)
                inproj(8 + c, pcg)
                b_ = bx[c]
                k.op("act", lambda: act.activation(out=b_[:], in_=px[:], func=AF.Copy), [px], [b_])
                if t > 0:
                    k.op("pool", lambda: pool.tensor_copy(out=uu[:, c, 0:2], in_=uup[:, c, TT:TT + 2]), [uup], [uu])
                k.op("dve", lambda: dve.tensor_tensor(out=uu[:, c, 2:2 + TT], in0=pcg[:], in1=b_[:], op=ALU.mult),
                     [pcg, b_], [uu])
                ca = cacc[c]
                k.op("dve", lambda: dve.tensor_scalar(out=ca[:], in0=uu[:, c, 0:TT], scalar1=P("cbw", l, c * 3 + 0),
                                                      scalar2=None, op0=ALU.mult), [uu, Pm], [ca])
                k.op("dve", lambda: dve.scalar_tensor_tensor(out=ca[:], in0=uu[:, c, 1:1 + TT], scalar=P("cbw", l, c * 3 + 1),
                                                             in1=ca[:], op0=ALU.mult, op1=ALU.add), [uu, Pm, ca], [ca])
                k.op("dve", lambda: dve.scalar_tensor_tensor(out=ca[:], in0=uu[:, c, 2:2 + TT], scalar=P("cbw", l, c * 3 + 2),
                                                             in1=ca[:], op0=ALU.mult, op1=ALU.add), [uu, Pm, ca], [ca])
                k.op("dve", lambda: dve.tensor_tensor(out=bo[:, c, :], in0=pbg[:], in1=ca[:], op=ALU.mult), [pbg, ca], [bo])
            if t == NTILE - 1:
                store(bpT[l].rearrange("(c p) n -> p c n", p=128), uu[:, :, TT:TT + 2], [uu])
            if t + 1 < NTILE:
                m1_pre(t + 1, 2)
            for ci in range(12):
                oc = 10 + ci
                pz = psz[ci % 4]
                inproj(oc, pz)
                qb = qkb[ci % 4]
                sc_ = 0.125 if ci < 4 else 1.0
                k.op("act", lambda: act.activation(out=qb[:], in_=pz[:], func=AF.Copy, scale=sc_), [pz], [qb])
                k.dma("act", qkvT[ci, :, t0:t0 + TT], qb[:], reads=[qb])
                if ci >= 4 and t0 >= S - KEEP:
                    kv = kv32[ci % 3]
                    k.op("dve", lambda: dve.tensor_copy(out=kv[:], in_=pz[:]), [pz], [kv])
                    dst = (kpT if ci < 8 else vpT)[l, (ci % 4) * 128:(ci % 4 + 1) * 128, t0 - (S - KEEP): t0 - (S - KEEP) + TT]
                    k.dma("sp", dst, kv[:], reads=[kv])
            if t == NTILE - 1:
                m1_tails(t)
            chk(100 + t)

        zv = lambda a, b: zs[:, a:b, :].rearrange("p c (s t) -> p c s t", t=NST)
        k.op("act", lambda: act.activation(out=sgs[:], in_=zs[:, 2:4, :], func=AF.Sigmoid), [zs], [sgs])
        k.op("dve", lambda: dve.tensor_tensor(out=xpa[:, :, :, 30:34], in0=zv(0, 2),
                                              in1=sgs[:].rearrange("p c (s t) -> p c s t", t=NST), op=ALU.mult), [zs, sgs], [xpa])
        for c in range(2):
            store(asT[l, c * 128:(c + 1) * 128, :, :], xpa[:, c, :, 4:34], [xpa])
        for c in range(2):
            in0 = AP(xpa, c * NSQ * 34, [[2 * NSQ * 34, 128], [34, NSQ], [1, NST], [1, 31]])
            in1 = AP(Pm, l * PL_COLS + POFF["caw"] + c * 31, [[2 * PL_COLS, 128], [0, NSQ], [0, NST], [1, 31]])
            k.op("dve", lambda: dve.tensor_tensor(out=tmpc[:].rearrange("p (s t j) -> p s t j", s=NSQ, t=NST), in0=in0, in1=in1,
                                                  op=ALU.mult), [xpa, Pm], [tmpc])
            k.op("dve", lambda: dve.tensor_reduce(out=ysm[:, c, :], in_=tmpc[:].rearrange("p (n j) -> p n j", j=31),
                                                  axis=AX.X, op=ALU.add), [tmpc], [ysm])
            cb = P("cab", l, c)
            k.op("act", lambda: act.activation(out=y32[:, c, 0:NS], in_=ysm[:, c, :], func=AF.Identity, bias=cb, scale=1.0),
                 [ysm, Pm], [y32])
            k.op("act", lambda: act.activation(out=ysq[:, c, 0:NS], in_=ysm[:, c, :], func=AF.Square, bias=cb, scale=1.0),
                 [ysm, Pm], [ysq])
            k.op("dve", lambda: dve.tensor_copy(out=ybf[:, c, 0:NS], in_=y32[:, c, 0:NS]), [y32], [ybf])
        a_tail(NS, mixs, lambda c: mixs[:, c, :])
        k.op("dve", lambda: dve.tensor_tensor(out=xpb[:, :, :, 2:6], in0=zv(8, 10), in1=zv(4, 6), op=ALU.mult), [zs], [xpb])
        for c in range(2):
            store(bsT[l, c * 128:(c + 1) * 128, :, :], xpb[:, c, :, 4:6], [xpb])
        for c in range(2):
            cv_ = cas[:, c, :].rearrange("p (s t) -> p s t", t=NST)
            k.op("dve", lambda: dve.tensor_scalar(out=cv_, in0=xpb[:, c, :, 0:4], scalar1=P("cbw", l, c * 3 + 0), scalar2=None,
                                                  op0=ALU.mult), [xpb, Pm], [cas])
            for tap in (1, 2):
                k.op("dve", lambda: dve.scalar_tensor_tensor(out=cv_, in0=xpb[:, c, :, tap:tap + 4], scalar=P("cbw", l, c * 3 + tap),
                                                             in1=cv_, op0=ALU.mult, op1=ALU.add), [xpb, Pm, cas], [cas])
            k.op("dve", lambda: dve.tensor_tensor(out=bo[:, c, 0:NS], in0=zs[:, 6 + c, :], in1=cas[:, c, :], op=ALU.mult),
                 [zs, cas], [bo])
        b_tail(NS, mixs, lambda c: mixs[:, 2 + c, :])
        k.op("act", lambda: act.activation(out=qs[:], in_=zs[:, 10:14, :], func=AF.Copy, scale=0.125), [zs], [qs])
        k.op("act", lambda: act.activation(out=ksn[:], in_=zs[:, 14:18, :], func=AF.Copy), [zs], [ksn])
        k.op("act", lambda: act.activation(out=vsn[:], in_=zs[:, 18:22, :], func=AF.Copy), [zs], [vsn])
        store(ksT[l].rearrange("(c p) n -> p c n", p=128), zs[:, 14:18, :], [zs])
        store(vsT[l].rearrange("(c p) n -> p c n", p=128), zs[:, 18:22, :], [zs])
        k.pop()
        chk(10)
        if stop_after == "M1":
            break


        def wdn(j, c0, n):
            return bigW[:, j * D + c0: j * D + c0 + n]

        k.push()
        NSLOT = 3 * (S // 128)
        qkv_sb = [[k.sb("qkv%d_%d" % (i, r), [128, S], BF16) for i in range(3)] for r in range(2)]
        Vp = k.sb("Vp", [128, NSLOT, 192], BF16)
        accsets = [[k.sb("acc%d_%d" % (a_, h), [128, 2048], F32) for h in range(2)] for a_ in range(2)]
        uctr = [0]
        gid = [0]
        caster = Caster(cast_pieces(l))
        pts = [k.sb("pt%d" % i, [128, 2, 128], BF16) for i in range(5)]
        lz = [k.sb("lz%d" % i, [128, TT], F32) for i in range(2)]
        rz = [k.sb("rz%d" % i, [128, TT], F32) for i in range(2)]
        onb = [k.sb("onb%d" % i, [128, TT], BF16) for i in range(2)]
        ps_s = [k.ps("ps_s%d" % i, [128, 4, 128]) for i in range(4)]
        ps_o = [k.ps("ps_o%d" % i, [128, 4, 128]) for i in range(2)]
        ps_t = [k.ps("ps_t%d" % i, [128, 4, 256], BF16) for i in range(2)]
        k.op("pool", lambda: pool.memset(Vp[:, :, 64:128], 1.0), [], [Vp])
        perms = [(1, 1), (4, 4), (16, 16)]
        slot_of = {}
        slots = []
        for (d, nr) in perms:
            nb = S // (128 * d)
            for r in range(nr):
                for i in range(nb):
                    slot_of[(d, r, i)] = len(slots)
                    slots.append((d, r, i))

        def tok(d, r, i):
            st = r + d * 128 * i
            return st, st + 127 * d + 1, d

        for hp in range(4):
            qT_, kT_, vT_ = qkv_sb[hp % 2]
            k.dma("sp", qT_[:], qkvT[hp, :, :], writes=[qT_])
            k.dma("sp", kT_[:], qkvT[4 + hp, :, :], writes=[kT_])
            k.dma("sp", vT_[:], qkvT[8 + hp, :, :], writes=[vT_])
            for g in range(NSLOT // 4):
                pt_ = ps_t[g % 2]
                for q in range(4):
                    d, r, i = slots[g * 4 + q]
                    a, b, st = tok(d, r, i)
                    k.op("pe", lambda: pe.transpose(out=pt_[:, q, 0:128], in_=vT_[:, a:b:st], identity=identb[:]),
                         [vT_, identb], [pt_], inc=(q == 3))
                dst = AP(Vp, g * 4 * 192, [[NSLOT * 192, 128], [192, 4], [128, 2], [1, 64]])
                src = pt_[:, :, 0:128].rearrange("p q (h d) -> p q h d", h=2)
                if g % 2 == 0:
                    k.op("dve", lambda: dve.tensor_copy(out=dst, in_=src), [pt_], [Vp])
                else:
                    k.op("act", lambda: act.activation(out=dst, in_=src, func=AF.Copy), [pt_], [Vp])
            units = []
            for hf in range(NHALF):
                aset = accsets[hf % 2]
                for h in range(2):
                    acc = aset[h]
                    groups = []
                    for g in range(4):
                        b0 = hf * 16 + g * 4
                        groups.append(([(1, 0, b0 + q) for q in range(4)],
                                       acc[:, g * 512:(g + 1) * 512].rearrange("p (q j) -> p q j", q=4), True))
                    for g in range(4):
                        i = hf * 4 + g
                        groups.append(([(4, r, i) for r in range(4)],
                                       acc[:, g * 512:(g + 1) * 512].rearrange("p (j r) -> p r j", r=4), False))
                    for g in range(4):
                        groups.append(([(16, g * 4 + q, hf) for q in range(4)],
                                       acc[:, :].rearrange("p (j r) -> p r j", r=16)[:, g * 4:g * 4 + 4, :], False))
                    for (ul, accv, first) in groups:
                        gid[0] += 1
                        for q, (d, r, i) in enumerate(ul):
                            units.append(dict(h=h, q=q, d=d, r=r, i=i, accv=accv, first=first, acc=acc, g=gid[0], hf=hf, last=False))
                units[-1]["last"] = True

            def front(u):
                u["ui"] = uctr[0]
                uctr[0] += 1
                pss_ = ps_s[u["ui"] % 4]
                ptt = pts[u["ui"] % 5]
                d, r, i, h = u["d"], u["r"], u["i"], u["h"]
                hs = slice(h * 64, (h + 1) * 64)
                kbs = ([(0, (d, r, i - 1))] if i > 0 else []) + [(1, (d, r, i))]
                u["kbs"] = kbs
                kb0 = kbs[0][0]
                qa, qb_, qst = tok(d, r, i)
                for (pos, blk) in kbs:
                    ka, kb_, kst = tok(*blk)
                    k.op("pe", lambda: pe.matmul(pss_[:, pos, :], lhsT=kT_[hs, ka:kb_:kst], rhs=qT_[hs, qa:qb_:qst],
                                                 start=True, stop=True), [kT_, qT_], [pss_], inc=(pos == 1))
                k.op("act", lambda: act.activation(out=ptt[:, kb0:2, :], in_=pss_[:, kb0:2, :], func=AF.Exp), [pss_], [ptt])
                if u["ui"] % 2 == 0:
                    k.op("pool", lambda: pool.tensor_tensor(out=ptt[:, kb0:2, :], in0=ptt[:, kb0:2, :], in1=M2b[:, kb0:2, :],
                                                            op=ALU.mult), [ptt, M2b], [ptt])
                else:
                    k.op("dve", lambda: dve.tensor_tensor(out=ptt[:, kb0:2, :], in0=ptt[:, kb0:2, :], in1=M2b[:, kb0:2, :],
                                                          op=ALU.mult), [ptt, M2b], [ptt])

            def back(u):
                ptt = pts[u["ui"] % 5]
                po = ps_o[u["g"] % 2]
                h, q, kbs = u["h"], u["q"], u["kbs"]
                for n_, (pos, blk) in enumerate(kbs):
                    sl_ = slot_of[blk]
                    k.op("pe", lambda: pe.matmul(po[:, q, :], lhsT=Vp[:, sl_, h * 64:h * 64 + 128], rhs=ptt[:, pos, :],
                                                 start=(n_ == 0), stop=(n_ == len(kbs) - 1)), [Vp, ptt], [po],
                         inc=(n_ == len(kbs) - 1))
                if q == 3:
                    accv, acc = u["accv"], u["acc"]
                    if u["first"]:
                        k.op("dve", lambda: dve.tensor_copy(out=accv, in_=po[:]), [po], [acc])
                    else:
                        k.op("dve", lambda: dve.tensor_tensor(out=accv, in0=po[:], in1=accv, op=ALU.add), [po, acc], [acc])

            def normalize(hf):
                aset = accsets[hf % 2]
                base = hf * 2048
                for pc in range(4):
                    cs = slice(pc * 512, (pc + 1) * 512)
                    l_, r_, o_ = lz[pc % 2], rz[pc % 2], onb[pc % 2]
                    k.op("act", lambda: act.activation(out=l_[0:64, :], in_=aset[0][64:128, cs], func=AF.Ln), [aset[0]], [l_])
                    k.op("act", lambda: act.activation(out=l_[64:128, :], in_=aset[1][0:64, cs], func=AF.Ln), [aset[1]], [l_])
                    k.op("act", lambda: act.activation(out=r_[:], in_=l_[:], func=AF.Exp, scale=-1.0), [l_], [r_])
                    k.op("dve", lambda: dve.tensor_tensor(out=o_[0:64, :], in0=aset[0][0:64, cs], in1=r_[0:64, :], op=ALU.mult),
                         [aset[0], r_], [o_])
                    k.op("dve", lambda: dve.tensor_tensor(out=o_[64:128, :], in0=aset[1][64:128, cs], in1=r_[64:128, :], op=ALU.mult),
                         [aset[1], r_], [o_])
                    k.dma("sp", mixT[4 + hp, :, base + pc * 512: base + (pc + 1) * 512], o_[:], reads=[o_])

            DEPTH = 3
            for idx in range(len(units) + DEPTH):
                if idx < len(units):
                    front(units[idx])
                if idx >= DEPTH:
                    ub = units[idx - DEPTH]
                    back(ub)
                    if ub["last"]:
                        normalize(ub["hf"])
                if idx % 8 == 0 and not caster.done():
                    caster.step()
        while not caster.done():
            caster.step()
        k.pop()
        chk(20)

        k.push()
        Kcl = [k.sb("Kc%d" % i, [128, 8, 512], F32) for i in range(2)]
        Vcl = [k.sb("Vc%d" % i, [128, 8, 512], F32) for i in range(2)]
        Kcb = k.sb("Kcb", [128, 8, 512], BF16)
        KTs = k.sb("KTs", [128, 4, 8, 128], BF16)
        Vca = k.sb("Vca", [128, 8, 8, 128], BF16)
        Vna = k.sb("Vna", [128, 8, 128], BF16)
        ptss = k.sb("ptss", [128, 8, 9, NST], BF16)
        lzs = k.sb("lzs", [128, 4, NST], F32)
        rzs = k.sb("rzs", [128, 4, NST], F32)
        ps_ss = k.ps("ps_ss", [128, 8, 9, NST])
        ps_so = k.ps("ps_so", [128, 8, NST])
        ps_tt = [k.ps("ps_tt%d" % i, [128, 4, 256], BF16) for i in range(2)]
        for i_ in range(2):
            k.op("pool", lambda: pool.memset(Kcl[i_][:], 0.0), [], [Kcl[i_]])
            k.op("pool", lambda: pool.memset(Vcl[i_][:], 0.0), [], [Vcl[i_]])

        def load_cache(s_i):
            for (Cc, src) in ((Kcl[s_i % 2], ck), (Vcl[s_i % 2], cv)):
                k.dma("sp", Cc[:, 0:4, :], src[l, s_i, 1536:2048, :].rearrange("(a p) f -> p a f", p=128), writes=[Cc], par=True)
                for m_ in range(4):
                    k.dma("sp", Cc[0:96, 4 + m_, :], src[l, s_i, m_:1536:16, :], writes=[Cc], par=True)
        load_cache(0)
        k.op("pool", lambda: pool.memset(Vca[:], 1.0), [], [Vca])
        k.op("pool", lambda: pool.memset(Vna[:], 1.0), [], [Vna])
        for s_i in range(NSQ):
            cols = slice(s_i * NST, (s_i + 1) * NST)
            Kc, Vc = Kcl[s_i % 2], Vcl[s_i % 2]
            if s_i + 1 < NSQ:
                load_cache(s_i + 1)
            k.op("act", lambda: act.activation(out=Kcb[:], in_=Kc[:], func=AF.Copy), [Kc], [Kcb])
            dst_e = AP(Vca, 0, [[8 * 8 * 128, 128], [1024, 8], [256, 4], [1, 64]])
            src_e = AP(Vc, 0, [[8 * 512, 128], [512, 8], [128, 4], [1, 64]])
            dst_o = AP(Vca, 128 + 64, [[8 * 8 * 128, 128], [1024, 8], [256, 4], [1, 64]])
            src_o = AP(Vc, 64, [[8 * 512, 128], [512, 8], [128, 4], [1, 64]])
            k.op("dve", lambda: dve.tensor_copy(out=dst_e, in_=src_e), [Vc], [Vca])
            k.op("dve", lambda: dve.tensor_copy(out=dst_o, in_=src_o), [Vc], [Vca])
            gi2 = 0
            for hp in range(4):
                for g in range(2):
                    pt_ = ps_tt[gi2 % 2]
                    for q in range(4):
                        k.op("pe", lambda: pe.transpose(out=pt_[:, q, 0:128], in_=Kcb[:, g * 4 + q, hp * 128:(hp + 1) * 128],
                                                        identity=identb[:]), [Kcb, identb], [pt_], inc=(q == 3))
                    if gi2 % 2 == 0:
                        k.op("dve", lambda: dve.tensor_copy(out=KTs[:, hp, g * 4:g * 4 + 4, :], in_=pt_[:, :, 0:128]), [pt_], [KTs])
                    else:
                        k.op("act", lambda: act.activation(out=KTs[:, hp, g * 4:g * 4 + 4, :], in_=pt_[:, :, 0:128], func=AF.Copy),
                             [pt_], [KTs])
                    gi2 += 1
            pt_ = ps_tt[0]
            for hp in range(4):
                k.op("pe", lambda: pe.transpose(out=pt_[0:NST, hp, 0:128], in_=vsn[:, hp, cols], identity=identb[:]),
                     [vsn, identb], [pt_], inc=(hp == 3))
            k.op("dve", lambda: dve.tensor_copy(out=AP(Vna, 0, [[1024, NST], [256, 4], [1, 64]]),
                                                in_=AP(pt_, 0, [[1024, NST], [256, 4], [1, 64]])), [pt_], [Vna])
            k.op("dve", lambda: dve.tensor_copy(out=AP(Vna, 192, [[1024, NST], [256, 4], [1, 64]]),
                                                in_=AP(pt_, 64, [[1024, NST], [256, 4], [1, 64]])), [pt_], [Vna])
            k.op("dve", lambda: dve.memset(ps_ss[:], 0.0), [], [ps_ss])
            for h in range(8):
                hp, hh = h // 2, h % 2
                hs = slice(hh * 64, hh * 64 + 64)
                for blk in range(8):
                    k.op("pe", lambda: pe.matmul(ps_ss[:, h, blk, :], lhsT=KTs[hs, hp, blk, :], rhs=qs[hs, hp, cols],
                                                 start=True, stop=True), [KTs, qs], [ps_ss], inc=False)
                k.op("pe", lambda: pe.matmul(ps_ss[0:NST, h, 8, :], lhsT=ksn[hs, hp, cols], rhs=qs[hs, hp, cols],
                                             start=True, stop=True), [ksn, qs], [ps_ss], inc=(h == 7))
            k.op("act", lambda: act.activation(out=ptss[:], in_=ps_ss[:], func=AF.Exp), [ps_ss], [ptss])
            msk = AP(Msb, 0, [[9 * NST, 128], [0, 8], [1, 9 * NST]])
            k.op("dve", lambda: dve.tensor_tensor(out=ptss[:].rearrange("p h b q -> p h (b q)"),
                                                  in0=ptss[:].rearrange("p h b q -> p h (b q)"), in1=msk, op=ALU.mult),
                 [ptss, Msb], [ptss])
            for h in range(8):
                for blk in range(8):
                    k.op("pe", lambda: pe.matmul(ps_so[:, h, :], lhsT=Vca[:, blk, h, :], rhs=ptss[:, h, blk, :],
                                                 start=(blk == 0), stop=False), [Vca, ptss], [ps_so], inc=False)
                k.op("pe", lambda: pe.matmul(ps_so[:, h, :], lhsT=Vna[0:NST, h, :], rhs=ptss[0:NST, h, 8, :],
                                             start=False, stop=True), [Vna, ptss], [ps_so], inc=(h == 7))
            k.op("act", lambda: act.activation(out=lzs[0:64, :, :], in_=ps_so[64:128, 0:8:2, :], func=AF.Ln), [ps_so], [lzs])
            k.op("act", lambda: act.activation(out=lzs[64:128, :, :], in_=ps_so[0:64, 1:8:2, :], func=AF.Ln), [ps_so], [lzs])
            k.op("act", lambda: act.activation(out=rzs[:], in_=lzs[:], func=AF.Exp, scale=-1.0), [lzs], [rzs])
            k.op("dve", lambda: dve.tensor_tensor(out=mixs[0:64, 4:8, cols], in0=ps_so[0:64, 0:8:2, :], in1=rzs[0:64, :, :],
                                                  op=ALU.mult), [ps_so, rzs], [mixs])
            k.op("dve", lambda: dve.tensor_tensor(out=mixs[64:128, 4:8, cols], in0=ps_so[64:128, 1:8:2, :], in1=rzs[64:128, :, :],
                                                  op=ALU.mult), [ps_so, rzs], [mixs])
        k.pop()
        chk(25)

        k.push()
        wo = k.sb("wo", [128, KC, D], BF16)
        wov = wo_bf[l].rearrange("(kc p) n -> p kc n", p=128)
        for kc in range(0, KC, 4):
            k.dma("sp", wo[:, kc:kc + 4, :], wov[:, kc:kc + 4, :], reads=[Two[l]], writes=[wo], par=True)
        xtl = [k.sb("xt3_%d" % i, [128, KC, TT], F32) for i in range(2)]
        mxl = [k.sb("mxl%d" % i, [128, KC, TT], BF16) for i in range(2)]
        csq = k.sb("csq", [128, 4, TT], BF16)
        o32l = [k.sb("o32_0", [128, KC, TT], F32)] * 2
        osql = [k.sb("osq_0", [128, KC, TT], BF16)] * 2
        sq3 = k.sb("sq3", [128, KC, TT], BF16)
        h2o = [k.sb("h2o0", [128, KC, TT], BF16)] * 2
        tmp3 = [k.sb("tmp3_%d" % i, [128, TT], F32) for i in range(2)]
        lnv = k.sb("lnv_3", [128, TT], F32)
        rstd = k.sb("rstd_3", [128, TT], F32)
        lnvb = k.sb("lnvb_3", [128, TT], F32)
        rstdb = k.sb("rstdb_3", [128, TT], F32)
        lnvc = k.sb("lnvc_3", [128, TT], F32)
        rstdc = k.sb("rstdc_3", [128, TT], F32)
        psn = k.ps("psn3", [128, TT])
        psn2 = k.ps("psn3b", [128, TT])
        pso = [k.ps("pso%d" % i, [128, TT]) for i in range(4)]
        wdv = wd_bf[l].rearrange("(j p) n -> p j n", p=128)
        for j in range(0, NJ, 2):
            k.dma("sp", bigW[:, j * D:(j + 2) * D].rearrange("p (j n) -> p j n", j=2), wdv[:, j:j + 2, :], reads=[Twd[l]], writes=[bigW], par=True)
        m0b = M0(1, 3, bf=True) if l == 0 else None
        if m0b is not None:
            for _ in range(3):
                m0b.load()

        def m3_front(n, mx):
            k.op("act", lambda: act.activation(out=csq[:, :, 0:n], in_=mx[:, 4:8, 0:n], func=AF.Square), [mx], [csq])
            rstd_bcast([csq[:, c, 0:n] for c in range(4)], n, 512, psn, lnv, rstd, [csq])
            for c in range(4):
                k.op("dve", lambda: dve.scalar_tensor_tensor(out=mx[:, 4 + c, 0:n], in0=mx[:, 4 + c, 0:n], scalar=P("goc", l, c),
                                                             in1=rstd[:, 0:n], op0=ALU.mult, op1=ALU.mult), [mx, Pm, rstd], [mx])

        o32d = [k.view(o32l[0]) for _ in range(KC)]
        osqd = [k.view(osql[0]) for _ in range(KC)]

        def m3_mid(n, mx, o32, osq):
            for dc in range(KC):
                po = pso[dc % 4]
                for kc in range(KC):
                    k.op("pe", lambda: pe.matmul(po[:, 0:n], lhsT=wo[:, kc, dc * 128:(dc + 1) * 128], rhs=mx[:, kc, 0:n],
                                                 start=(kc == 0), stop=(kc == KC - 1)), [wo, mx], [po], inc=(kc == KC - 1))
                k.op("act", lambda: act.activation(out=o32[:, dc, 0:n], in_=po[:, 0:n], func=AF.Copy), [po], [o32d[dc]])
                k.op("act", lambda: act.activation(out=osq[:, dc, 0:n], in_=po[:, 0:n], func=AF.Square), [po], [osqd[dc]])

        def load_mx(t):
            t0 = t * TT
            k.dma("sp", mxl[t % 2][:], mixT[:, :, t0:t0 + TT].rearrange("c p n -> p c n"), writes=[mxl[t % 2]])

        def load_xt(t):
            t0 = t * TT
            k.dma("sp", xtl[t % 2][:], xin.rearrange("(kc p) n -> p kc n", p=128)[:, :, t0:t0 + TT], writes=[xtl[t % 2]])

        load_mx(0)
        load_xt(0)
        m3_front(TT, mxl[0])
        for t in range(NTILE):
            t0 = t * TT
            mx, xt, o32, osq = mxl[t % 2], xtl[t % 2], o32l[t % 2], osql[t % 2]
            if t + 1 < NTILE:
                load_xt(t + 1)
            m3_mid(TT, mx, o32, osq)
            if t + 1 < NTILE:
                load_mx(t + 1)
                m3_front(TT, mxl[(t + 1) % 2])
            postnorm_resid(TT, o32, osq, psn2, lnvb, rstdb, 2, xt, lambda dc: xt[:, dc, :], False, o32d, osqd)
            k.dma("act", x1T.rearrange("(kc p) n -> p kc n", p=128)[:, :, t0:t0 + TT], xt[:], reads=[xt])
            hh = h2o[t % 2]
            prenorm(xt, lambda kc: xt[:] if kc is None else xt[:, kc, :], TT, sq3, hh, 3, 4, False, psn2, lnvc, rstdc, tmp3)
            k.dma("act", h2T[:, :, t0:t0 + TT].rearrange("c p n -> p c n"), hh[:], reads=[hh])
            if m0b is not None:
                for _ in range(6):
                    m0b.mm()
                    m0b.load()
        m3_front(NS, mixs)
        m3_mid(NS, mixs, o32l[0], osql[0])
        postnorm_resid(NS, o32l[0], osql[0], psn2, lnvb, rstdb, 2, xs, None, True, o32d, osqd)
        k.pop()
        chk(30)

        k.push()
        ST = 1024
        NSUP = S // ST
        xt = k.sb("xtf", [128, KC, TT], F32)
        sqb = k.sb("sqbf", [128, KC, TT], BF16)
        h2l = [k.sb("h2_%d" % i, [128, KC, TT], BF16) for i in range(2)]
        fT = k.sb("fT", [128, NJ, ST], BF16)
        wgb = [k.sb("wgb%d" % i, [128, KC, 128], BF16) for i in range(3)]
        wub = [k.sb("wub%d" % i, [128, KC, 128], BF16) for i in range(3)]
        g32 = [k.sb("g32_%d" % i, [128, 2 + TT], F32) for i in range(4)]
        cc = [k.sb("cc%d" % i, [128, TT], F32) for i in range(2)]
        sl = [k.sb("sl%d" % i, [128, TT], F32) for i in range(2)]
        o32 = k.sb("o32f", [128, KC, TT], F32)
        lnv = k.sb("lnv_f", [128, TT], F32)
        rstd = k.sb("rstd_f", [128, TT], F32)
        gus = k.sb("gus", [128, 2, NJ, NS], F32)
        cws = k.sb("cws", [128, 3, NJ, NS], F32)
        cfs = k.sb("cfs", [128, NJ, NS], F32)
        fTs = k.sb("fTs", [128, NJ, NS], BF16)
        psn = k.ps("psnf", [128, TT])
        psg = [k.ps("psg%d" % i, [128, TT]) for i in range(2)]
        psu = [k.ps("psu%d" % i, [128, TT]) for i in range(2)]
        psd = [k.ps("psd%d" % i, [128, TT]) for i in range(2)]
        pssf = k.ps("pssf", [128, TT])
        o32fd = [k.view(o32) for _ in range(KC)]
        sqfd = [k.view(sqb) for _ in range(KC)]
        k.op("pool", lambda: pool.memset(ghalo[:], 0.0), [], [ghalo])
        k.dma("sp", xpf[:, :, :, 0:2], sf[:, l, :, :, :], writes=[xpf])
        prenorm(xs, lambda kc: xs[:] if kc is None else xs[:, kc, :], NS, sqs, hts, 3, 4, True, psn, lnv, rstd, [])
        wgv = wg_bf[l].rearrange("(kc p) n -> p kc n", p=128)
        wuv = wu_bf[l].rearrange("(kc p) n -> p kc n", p=128)
        wi = 0
        un = 0
        for st_ in range(NSUP):
            s0 = st_ * ST
            for hf in range(2):
                k.dma("pool", h2l[hf][:], h2T[:, :, s0 + hf * TT: s0 + (hf + 1) * TT].rearrange("c p n -> p c n"), writes=[h2l[hf]])
            for j in range(NJ):
                wg_, wu_ = wgb[wi % 3], wub[wi % 3]
                wi += 1
                k.dma("sp", wg_[:], wgv[:, :, j * 128:(j + 1) * 128], reads=[Twg[l]], writes=[wg_])
                k.dma("sp", wu_[:], wuv[:, :, j * 128:(j + 1) * 128], reads=[Twu[l]], writes=[wu_])
                gprev = None
                for hf in range(2):
                    h2 = h2l[hf]
                    g_, c_, s_ = g32[un % 4], cc[un % 2], sl[un % 2]
                    un += 1
                    for kc in range(KC):
                        k.op("pe", lambda: pe.matmul(psg[hf][:], lhsT=wg_[:, kc, :], rhs=h2[:, kc, :],
                                                     start=(kc == 0), stop=(kc == KC - 1)), [wg_, h2], [psg[hf]], inc=(kc == KC - 1))
                    for kc in range(KC):
                        k.op("pe", lambda: pe.matmul(psu[hf][:], lhsT=wu_[:, kc, :], rhs=h2[:, kc, :],
                                                     start=(kc == 0), stop=(kc == KC - 1)), [wu_, h2], [psu[hf]], inc=(kc == KC - 1))
                    if st_ == 0 and hf == 0:
                        for (w_, gi_) in ((wg_, 0), (wu_, 1)):
                            for kc in range(KC):
                                k.op("pe", lambda: pe.matmul(pssf[:, 0:NS], lhsT=w_[:, kc, :], rhs=hts[:, kc, :], start=(kc == 0),
                                                             stop=(kc == KC - 1)), [w_, hts], [pssf], inc=(kc == KC - 1))
                            k.op("dve", lambda: dve.tensor_copy(out=gus[:, gi_, j, :], in_=pssf[:, 0:NS]), [pssf], [gus])
                    if hf == 0:
                        k.op("pool", lambda: pool.tensor_copy(out=g_[:, 0:2], in_=ghalo[:, j, :]), [ghalo], [g_])
                    else:
                        k.op("pool", lambda: pool.tensor_copy(out=g_[:, 0:2], in_=gprev[:, TT:TT + 2]), [gprev], [g_])
                    k.op("act", lambda: act.activation(out=g_[:, 2:2 + TT], in_=psg[hf][:], func=AF.Copy), [psg[hf]], [g_])
                    k.op("act", lambda: act.activation(out=c_[:], in_=psg[hf][:], func=AF.Identity,
                                                       scale=P("cfw", l, j * 3 + 2)), [psg[hf], Pm], [c_])
                    if hf == 1:
                        k.op("pool", lambda: pool.tensor_copy(out=ghalo[:, j, :], in_=g_[:, TT:TT + 2]), [g_], [ghalo])
                    k.op("dve", lambda: dve.scalar_tensor_tensor(out=c_[:], in0=g_[:, 1:1 + TT], scalar=P("cfw", l, j * 3 + 1), in1=c_[:],
                                                                 op0=ALU.mult, op1=ALU.add), [g_, Pm, c_], [c_])
                    k.op("dve", lambda: dve.scalar_tensor_tensor(out=c_[:], in0=g_[:, 0:TT], scalar=P("cfw", l, j * 3 + 0), in1=c_[:],
                                                                 op0=ALU.mult, op1=ALU.add), [g_, Pm, c_], [c_])
                    k.op("act", lambda: act.activation(out=s_[:], in_=c_[:], func=AF.Silu), [c_], [s_])
                    k.op("dve", lambda: dve.tensor_tensor(out=fT[:, j, hf * TT:(hf + 1) * TT], in0=psu[hf][:], in1=s_[:], op=ALU.mult),
                         [psu[hf], s_], [fT])
                    gprev = g_
            for hf in range(2):
                k.dma("pool", xt[:], x1T.rearrange("(kc p) n -> p kc n", p=128)[:, :, s0 + hf * TT: s0 + (hf + 1) * TT], writes=[xt])
                for dc in range(KC):
                    pd = psd[dc % 2]
                    for j in range(NJ):
                        k.op("pe", lambda: pe.matmul(pd[:], lhsT=wdn(j, dc * 128, 128), rhs=fT[:, j, hf * TT:(hf + 1) * TT],
                                                     start=(j == 0), stop=(j == NJ - 1)), [bigW, fT], [pd], inc=(j == NJ - 1))
                    k.op("act", lambda: act.activation(out=o32[:, dc, :], in_=pd[:], func=AF.Copy), [pd], [o32fd[dc]])
                    k.op("act", lambda: act.activation(out=sqb[:, dc, :], in_=pd[:], func=AF.Square), [pd], [sqfd[dc]])
                postnorm_resid(TT, o32, sqb, psn, lnv, rstd, 5, xt, lambda dc: xt[:, dc, :], False, o32fd, sqfd)
                k.dma("sp", xout.rearrange("(kc p) n -> p kc n", p=128)[:, :, s0 + hf * TT: s0 + (hf + 1) * TT], xt[:], reads=[xt])
        store(fpT[l].rearrange("(j p) n -> p j n", p=128), ghalo[:], [ghalo])
        k.op("dve", lambda: dve.tensor_copy(out=xpf[:, :, :, 2:6], in_=gus[:, 0, :, :].rearrange("p j (s t) -> p j s t", t=NST)),
             [gus], [xpf])
        for s_i in range(NSQ):
            store(fsT[l].rearrange("(j p) s n -> p j s n", p=128)[:, :, s_i, :], xpf[:, :, s_i, 4:6], [xpf])
        for tap in range(3):
            srcw = AP(Pm, l * PL_COLS + POFF["cfw"] + tap, [[2 * PL_COLS, 128], [3, NJ], [0, NS]])
            k.op("dve", lambda: dve.tensor_copy(out=cws[:, tap, :, :], in_=srcw), [Pm], [cws])
        cfv = cfs[:].rearrange("p j (s t) -> p j s t", t=NST)
        for tap in range(3):
            wv_ = cws[:, tap, :, :].rearrange("p j (s t) -> p j s t", t=NST)
            if tap == 0:
                k.op("dve", lambda: dve.tensor_tensor(out=cfv, in0=xpf[:, :, :, 0:4], in1=wv_, op=ALU.mult), [xpf, cws], [cfs])
            else:
                k.op("dve", lambda: dve.tensor_tensor(out=gus[:, 0, :, :].rearrange("p j (s t) -> p j s t", t=NST),
                                                      in0=xpf[:, :, :, tap:tap + 4], in1=wv_, op=ALU.mult), [xpf, cws], [gus])
                k.op("dve", lambda: dve.tensor_tensor(out=cfs[:], in0=cfs[:], in1=gus[:, 0, :, :], op=ALU.add), [cfs, gus], [cfs])
        k.op("act", lambda: act.activation(out=cfs[:], in_=cfs[:], func=AF.Silu), [cfs], [cfs])
        k.op("dve", lambda: dve.tensor_tensor(out=fTs[:], in0=cfs[:], in1=gus[:, 1, :, :], op=ALU.mult), [cfs, gus], [fTs])
        for dc in range(KC):
            pd = psd[dc % 2]
            for j in range(NJ):
                k.op("pe", lambda: pe.matmul(pd[:, 0:NS], lhsT=wdn(j, dc * 128, 128), rhs=fTs[:, j, :],
                                             start=(j == 0), stop=(j == NJ - 1)), [bigW, fTs], [pd], inc=(j == NJ - 1))
            k.op("act", lambda: act.activation(out=o32[:, dc, 0:NS], in_=pd[:, 0:NS], func=AF.Copy), [pd], [o32fd[dc]])
            k.op("act", lambda: act.activation(out=sqb[:, dc, 0:NS], in_=pd[:, 0:NS], func=AF.Square), [pd], [sqfd[dc]])
        postnorm_resid(NS, o32, sqb, psn, lnv, rstd, 5, xs, None, True, o32fd, sqfd)
        k.pop()
        chk(40)

    store(ysT.rearrange("(kc p) n -> p kc n", p=128), xs[:], [xs])
    k.barrier()
    k.close()
    return nc


def _fm(a):
    a = np.asarray(a, np.float32)
    F = a.shape[-1]
    b = a.reshape(a.shape[:-1] + (F // 128, 128))
    return np.ascontiguousarray(np.moveaxis(b, -1, 0))


def _pack_params(inp):
    out = np.zeros((128, 2 * PL_COLS), np.float32)
    for l in range(2):
        def put(name, arr):
            arr = np.asarray(arr, np.float32).reshape(128, -1)
            o = l * PL_COLS + POFF[name]
            out[:, o:o + arr.shape[1]] = arr
        put("gpm", _fm(inp["g_pre_mix"][l]))
        put("gpostm", _fm(inp["g_post_mix"][l]))
        put("gpref", _fm(inp["g_pre_ffn"][l]))
        put("gpostf", _fm(inp["g_post_ffn"][l]))
        put("bada", _fm(inp["b_ada"][l]))
        put("caw", np.transpose(_fm(inp["conv_a_w"][l]), (0, 2, 1)))
        put("cab", _fm(inp["conv_a_b"][l]))
        put("lng", _fm(inp["ln_a_g"][l]))
        put("lnb", _fm(inp["ln_a_b"][l]))
        put("goa", _fm(inp["g_out_a"][l]))
        put("cbw", np.transpose(_fm(inp["conv_b_w"][l]), (0, 2, 1)))
        put("gob", _fm(inp["g_out_b"][l]))
        put("goc", _fm(inp["g_out_c"][l]))
        put("cfw", np.transpose(_fm(inp["conv_f_w"][l]), (0, 2, 1)))
    return out


def _consts():
    c = np.zeros((128, C_COLS), np.float32)
    c[:, C_IDENT:C_IDENT + 128] = np.eye(128, dtype=np.float32)
    kk = np.arange(128)[:, None]
    jj = np.arange(128)[None, :]
    c[:, C_M2:C_M2 + 128] = (kk >= jj)
    c[:, C_M2 + 128:C_M2 + 256] = (kk <= jj)
    ms = np.zeros((128, 9, NST), np.float32)
    for blk in range(9):
        for p in range(128):
            if blk < 4:
                idx = 1536 + 128 * blk + p
            elif blk < 8:
                if p >= 96:
                    continue
                idx = 16 * p + (blk - 4)
            else:
                if p >= NST:
                    continue
                idx = 2048 + p
            for qi in range(NST):
                dl = 2048 + qi - idx
                if dl < 0:
                    continue
                m = (dl <= 128) + (dl % 4 == 0 and dl <= 512) + (dl % 16 == 0 and dl <= 2048)
                ms[p, blk, qi] = m
    c[:, C_MS:C_MS + 36] = ms.reshape(128, 36)
    return c


_NC_CACHE = {}


def make_in_maps(inp, cores, S):
    prm = _pack_params(inp)
    cst = _consts()
    shared = {k_: np.ascontiguousarray(np.asarray(inp[k_], np.float32)) for k_ in
              ("w_ada", "w_in", "w_o", "w_gate", "w_up", "w_down")}
    maps = []
    for b in cores:
        sl = slice(NSQ * b, NSQ * b + NSQ)
        m = dict(shared)
        m["xT"] = np.ascontiguousarray(np.asarray(inp["x_prompt"][b], np.float32)[:S].T)
        m["xsT"] = np.ascontiguousarray(np.asarray(inp["x_sample"][sl], np.float32).reshape(NS, D).T)
        cc = np.concatenate([np.asarray(inp["c_prompt"][b:b + 1]), np.asarray(inp["c_sample"][sl])], 0)
        m["cT"] = np.ascontiguousarray(np.transpose(_fm(cc), (0, 2, 1)))
        m["prm"] = prm
        m["cst"] = cst
        m["ck"] = np.ascontiguousarray(np.asarray(inp["cache_k"][:, sl], np.float32).reshape(2, NSQ, WBUF, 512))
        m["cv"] = np.ascontiguousarray(np.asarray(inp["cache_v"][:, sl], np.float32).reshape(2, NSQ, WBUF, 512))
        a = np.asarray(inp["state_conv_a"][:, sl], np.float32)
        m["sa"] = np.ascontiguousarray(np.transpose(a.reshape(2, NSQ, 30, 2, 128), (4, 0, 3, 1, 2)))
        a = np.asarray(inp["state_conv_b"][:, sl], np.float32)
        m["sbs"] = np.ascontiguousarray(np.transpose(a.reshape(2, NSQ, 2, 2, 128), (4, 0, 3, 1, 2)))
        a = np.asarray(inp["state_ffn_conv"][:, sl], np.float32)
        m["sf"] = np.ascontiguousarray(np.transpose(a.reshape(2, NSQ, 2, NJ, 128), (4, 0, 3, 1, 2)))
        maps.append(m)
    return maps


def kernel(**inp):
    S = 4096
    if "nc" not in _NC_CACHE:
        _NC_CACHE["nc"] = build(S)
    nc = _NC_CACHE["nc"]
    cores = list(range(8))
    maps = make_in_maps(inp, cores, S)
    res = run_bass_kernel_spmd(nc, maps, core_ids=cores)
    return assemble(res.results, S)


def assemble(results, S):
    nb = len(results)
    KEEP = min(2048, S)

    def g(name):
        return [np.asarray(r[name], np.float32) for r in results]
    y = np.stack([a.T for a in g("yT")])
    ys = np.concatenate([a.T.reshape(NSQ, NST, D) for a in g("ysT")], 0)
    kp = np.stack([np.transpose(a, (0, 2, 1)).reshape(2, KEEP, 8, 64) for a in g("kpT")], 1)
    vp = np.stack([np.transpose(a, (0, 2, 1)).reshape(2, KEEP, 8, 64) for a in g("vpT")], 1)
    ks = np.concatenate([np.transpose(a, (0, 2, 1)).reshape(2, NSQ, NST, 8, 64) for a in g("ksT")], 1)
    vs = np.concatenate([np.transpose(a, (0, 2, 1)).reshape(2, NSQ, NST, 8, 64) for a in g("vsT")], 1)
    ap_ = np.stack([np.transpose(a, (0, 2, 1)) for a in g("apT")], 1)
    as_ = np.concatenate([np.transpose(a, (0, 2, 3, 1)) for a in g("asT")], 1)
    bp = np.stack([np.transpose(a, (0, 2, 1)) for a in g("bpT")], 1)
    bs = np.concatenate([np.transpose(a, (0, 2, 3, 1)) for a in g("bsT")], 1)
    fp = np.stack([np.transpose(a, (0, 2, 1)) for a in g("fpT")], 1)
    fs = np.concatenate([np.transpose(a, (0, 2, 3, 1)) for a in g("fsT")], 1)
    return (y, ys, kp, vp, ks, vs, ap_, as_, bp, bs, fp, fs)
```
